# Optimizing a Trainium2 kernel written in Bass

```python
import jax, jax.numpy as jnp
from jax import lax
import numpy as np

D_MODEL = 1024
BATCH = 8
SEQ = 2048
DEPTH = 4
DEC_BATCH = 128
DEC_SEQ = 8
PAST_LEN = 16384
PAGE_SIZE = 128

N_MIXERS = 4
D_FF = 2816
FFN_RES = 0.5
CHUNK = 128
A_WIDTH = D_MODEL
A_GROUPS = 4
B_WIDTH = 3
C_WIDTH = 31
POOL_WINDOWS = (2, 4, 8, 16)
POOL_GROUPS = len(POOL_WINDOWS)
POOL_GROUP_DIM = D_MODEL // POOL_GROUPS
POOL_BUF = max(POOL_WINDOWS) - 1
ALPHA = (2 * DEPTH) ** 0.25
BETA = (8 * DEPTH) ** -0.25
LN_EPS = 1e-5

kernel_name = "hybrid_gmlp_conv_pool_deepnorm_step"


def layer_norm(x, g, b):
    xf = x.astype(jnp.float32)
    mu = jnp.mean(xf, axis=-1, keepdims=True)
    var = jnp.mean(jnp.square(xf - mu), axis=-1, keepdims=True)
    y = (xf - mu) * lax.rsqrt(var + LN_EPS)
    return (y * g.astype(jnp.float32) + b.astype(jnp.float32)).astype(x.dtype)


def swiglu(h, w_in, w_out):
    g, u = jnp.split(h @ w_in, 2, axis=-1)
    return (jax.nn.silu(g) * u) @ w_out


def causal_dw_conv(xx, w):
    channels = w.shape[1]
    return lax.conv_general_dilated(
        xx, w[:, None, :].astype(xx.dtype), window_strides=(1,), padding='VALID',
        dimension_numbers=('NWC', 'WIO', 'NWC'), feature_group_count=channels)


def mixer_chunk_mlp(h, w_in, b_in, ln_g, ln_b, w_s, b_s, w_out):
    bn, t_len, _ = h.shape
    z = jax.nn.gelu(h @ w_in + b_in)
    u, v = jnp.split(z, 2, axis=-1)
    v = layer_norm(v, ln_g, ln_b)
    n_chunks = -(-t_len // CHUNK)
    pad = n_chunks * CHUNK - t_len
    vp = jnp.pad(v, ((0, 0), (0, pad), (0, 0))).reshape(
        bn, n_chunks, CHUNK, A_GROUPS, A_WIDTH // A_GROUPS)
    causal = jnp.tril(jnp.ones((CHUNK, CHUNK), dtype=bool))
    w_m = jnp.where(causal[None], w_s, 0)
    s = jnp.einsum('gts,bnsgc->bntgc', w_m, vp) + b_s.T[None, None, :, :, None]
    s = s.reshape(bn, n_chunks * CHUNK, A_WIDTH)[:, :t_len]
    y = (u * s) @ w_out
    last = ((t_len - 1) // CHUNK) * CHUNK
    return y, v[:, last:]


def mixer_short_conv(h, buf, w_in, conv_w, w_out):
    bg, cg, xin = jnp.split(h @ w_in, 3, axis=-1)
    xx = jnp.concatenate([buf, cg * xin], axis=1)
    y = bg * causal_dw_conv(xx, conv_w)
    return y @ w_out, xx[:, -(B_WIDTH - 1):]


def mixer_conformer_conv(h, buf, w_pw1, b_pw1, dw, b_dw, ln_g, ln_b, w_pw2, b_pw2):
    a, g = jnp.split(h @ w_pw1 + b_pw1, 2, axis=-1)
    glu = a * jax.nn.sigmoid(g)
    xx = jnp.concatenate([buf, glu], axis=1)
    z = causal_dw_conv(xx, dw) + b_dw
    z = jax.nn.silu(layer_norm(z, ln_g, ln_b))
    return z @ w_pw2 + b_pw2, xx[:, -(C_WIDTH - 1):]


def mixer_pool(h, buf, start_pos, w_grp, b_grp, scale):
    bn, t_len, _ = h.shape
    p_len = buf.shape[1]
    xx = jnp.concatenate([buf, h], axis=1)
    cs = jnp.cumsum(xx.astype(jnp.float32), axis=1)
    pos = start_pos + jnp.arange(t_len)
    hf = h.astype(jnp.float32)
    ds = []
    for gi, w in enumerate(POOL_WINDOWS):
        sl = slice(gi * POOL_GROUP_DIM, (gi + 1) * POOL_GROUP_DIM)
        csg = cs[..., sl]
        hi = csg[:, p_len:]
        lo = jnp.pad(csg, ((0, 0), (w, 0), (0, 0)))[:, p_len:p_len + t_len]
        count = jnp.minimum(pos + 1, w).astype(jnp.float32)[None, :, None]
        ds.append((hi - lo) / count - hf[..., sl])
    d = jnp.stack(ds, axis=2).astype(h.dtype)
    y = jnp.einsum('btgc,gce->btge', d, w_grp) + b_grp
    return y.reshape(bn, t_len, D_MODEL) * scale, xx[:, -POOL_BUF:]


def run_trunk(x, c, buf3, buf31, bufpool, start_pos, shared, a_p, b_p, c_p, d_p):
    ada_w, ada_b, ln_g, ln_b, ffn_w_in, ffn_w_out = shared
    bn = x.shape[0]
    new_v = new3 = new31 = newpool = None
    for i in range(DEPTH):
        mod = (jax.nn.silu(c) @ ada_w[i] + ada_b[i]).reshape(bn, 3, 3, D_MODEL)

        def modulate(h, s):
            return h * (1 + mod[:, s, 1][:, None, :]) + mod[:, s, 0][:, None, :]

        def gate(y, s):
            return (1 + mod[:, s, 2][:, None, :]) * y

        f = swiglu(modulate(x, 0), ffn_w_in[i, 0], ffn_w_out[i, 0])
        x = layer_norm(ALPHA * x + FFN_RES * gate(f, 0), ln_g[i, 0], ln_b[i, 0])

        h = modulate(x, 1)
        kind = i % N_MIXERS
        if kind == 0:
            y, new_v = mixer_chunk_mlp(h, *a_p)
        elif kind == 1:
            y, new3 = mixer_short_conv(h, buf3, *b_p)
        elif kind == 2:
            y, new31 = mixer_conformer_conv(h, buf31, *c_p)
        else:
            y, newpool = mixer_pool(h, bufpool, start_pos, *d_p)
        x = layer_norm(ALPHA * x + gate(y, 1), ln_g[i, 1], ln_b[i, 1])

        f = swiglu(modulate(x, 2), ffn_w_in[i, 1], ffn_w_out[i, 1])
        x = layer_norm(ALPHA * x + FFN_RES * gate(f, 2), ln_g[i, 2], ln_b[i, 2])
    return x, new_v, new3, new31, newpool


def setup_inputs(seed: int = 0) -> dict:
    key = jax.random.key(seed)
    ks = iter(jax.random.split(key, 40))

    def nrm(shape, s):
        return jax.random.normal(next(ks), shape, jnp.float32) * s

    D = D_MODEL
    gd = POOL_GROUP_DIM
    return {
        "x_prompt": nrm((BATCH, SEQ, D), 1.0),
        "x_sample": nrm((DEC_BATCH, DEC_SEQ, D), 1.0),
        "c_prompt": nrm((BATCH, D), 1.0),
        "c_sample": nrm((DEC_BATCH, D), 1.0),
        "state_conv3": nrm((DEC_BATCH, B_WIDTH - 1, D), 0.5),
        "state_conv31": nrm((DEC_BATCH, C_WIDTH - 1, D), 0.5),
        "state_pool": nrm((DEC_BATCH, POOL_BUF, D), 1.0),
        "ada_w": nrm((DEPTH, D, 9 * D), 0.1 * D ** -0.5),
        "ada_b": nrm((DEPTH, 9 * D), 0.01),
        "ln_g": 1.0 + nrm((DEPTH, 3, D), 0.05),
        "ln_b": nrm((DEPTH, 3, D), 0.01),
        "ffn_w_in": nrm((DEPTH, 2, D, 2 * D_FF), D ** -0.5),
        "ffn_w_out": nrm((DEPTH, 2, D_FF, D), BETA * D_FF ** -0.5),
        "a_w_in": nrm((D, 2 * A_WIDTH), D ** -0.5),
        "a_b_in": nrm((2 * A_WIDTH,), 0.01),
        "a_ln_g": 1.0 + nrm((A_WIDTH,), 0.05),
        "a_ln_b": nrm((A_WIDTH,), 0.01),
        "a_w_s": nrm((A_GROUPS, CHUNK, CHUNK), CHUNK ** -0.5),
        "a_b_s": 1.0 + nrm((A_GROUPS, CHUNK), 0.01),
        "a_w_out": nrm((A_WIDTH, D), BETA * A_WIDTH ** -0.5),
        "b_w_in": nrm((D, 3 * D), D ** -0.5),
        "b_conv": nrm((B_WIDTH, D), B_WIDTH ** -0.5),
        "b_w_out": nrm((D, D), BETA * D ** -0.5),
        "c_w_pw1": nrm((D, 2 * D), D ** -0.5),
        "c_b_pw1": nrm((2 * D,), 0.01),
        "c_dw": nrm((C_WIDTH, D), C_WIDTH ** -0.5),
        "c_b_dw": nrm((D,), 0.01),
        "c_ln_g": 1.0 + nrm((D,), 0.05),
        "c_ln_b": nrm((D,), 0.01),
        "c_w_pw2": nrm((D, D), BETA * D ** -0.5),
        "c_b_pw2": nrm((D,), 0.01),
        "d_w_grp": nrm((POOL_GROUPS, gd, gd), BETA * gd ** -0.5),
        "d_b_grp": nrm((POOL_GROUPS, gd), 0.01),
        "d_scale": 1.0 + nrm((D,), 0.1),
    }


def reference(x_prompt, x_sample, c_prompt, c_sample, state_conv3, state_conv31, state_pool,
              ada_w, ada_b, ln_g, ln_b, ffn_w_in, ffn_w_out,
              a_w_in, a_b_in, a_ln_g, a_ln_b, a_w_s, a_b_s, a_w_out,
              b_w_in, b_conv, b_w_out,
              c_w_pw1, c_b_pw1, c_dw, c_b_dw, c_ln_g, c_ln_b, c_w_pw2, c_b_pw2,
              d_w_grp, d_b_grp, d_scale):
    shared = (ada_w, ada_b, ln_g, ln_b, ffn_w_in, ffn_w_out)
    a_p = (a_w_in, a_b_in, a_ln_g, a_ln_b, a_w_s, a_b_s, a_w_out)
    b_p = (b_w_in, b_conv, b_w_out)
    c_p = (c_w_pw1, c_b_pw1, c_dw, c_b_dw, c_ln_g, c_ln_b, c_w_pw2, c_b_pw2)
    d_p = (d_w_grp, d_b_grp, d_scale)

    dt = x_prompt.dtype
    z3 = jnp.zeros((BATCH, B_WIDTH - 1, D_MODEL), dt)
    z31 = jnp.zeros((BATCH, C_WIDTH - 1, D_MODEL), dt)
    zpool = jnp.zeros((BATCH, POOL_BUF, D_MODEL), dt)
    y_prompt, v_p, conv3_p, conv31_p, pool_p = run_trunk(
        x_prompt, c_prompt, z3, z31, zpool, 0, shared, a_p, b_p, c_p, d_p)

    y_sample, v_s, conv3_s, conv31_s, pool_s = run_trunk(
        x_sample, c_sample, state_conv3, state_conv31, state_pool, PAST_LEN,
        shared, a_p, b_p, c_p, d_p)

    return (y_prompt, y_sample, v_p, v_s, conv3_p, conv3_s, conv31_p, conv31_s, pool_p, pool_s)
```

```python
import contextlib
import numpy as np
import concourse.bass as bass
import concourse.mybir as mybir
from concourse.bass_utils import run_bass_kernel_spmd

F32 = mybir.dt.float32
BF16 = mybir.dt.bfloat16
AF = mybir.ActivationFunctionType
ALU = mybir.AluOpType

D = 1024
KC = 8
DFF = 2816
NJ = 22
NS = 16
SL = 8
NSAMP = NS * SL
DEPTH = 4
ALPHA = (2 * DEPTH) ** 0.25
LN_EPS = 1e-5
EPS_MAIN = LN_EPS / (ALPHA * ALPHA)
POOL_W = (2, 4, 8, 16)
KP = 8
PARTS = [list(range(a, min(a + KP, NJ))) for a in range(0, NJ, KP)]
GELU_C = 1.5957691216057308

PV = {}
_o = 0
for _n, _c in [("ada_b", 4 * 72), ("ln_g", 96), ("ln_b", 96), ("a_b_in", 16), ("a_ln_g", 8), ("a_ln_b", 8),
               ("b_conv", 24), ("c_b_pw1", 16), ("c_dw", 248), ("c_b_dw", 8), ("c_ln_g", 8), ("c_ln_b", 8),
               ("c_b_pw2", 8), ("d_b_grp", 8), ("d_scale", 8)]:
    PV[_n] = _o
    _o += _c
NPV = _o
MW = {"a_in": (0, 8), "a_out": (8, 4), "b_in": (12, 12), "b_out": (24, 4), "c_pw1": (28, 8), "c_pw2": (36, 4)}
NMW = 40


class Op:
    __slots__ = ("eng", "fn", "alldeps", "deps", "idx", "grp", "cnt", "val", "needed", "pos", "cost", "busy")


class _FakeInst:
    def then_inc(self, *a, **k):
        return self


class _FakeE:
    def __init__(self):
        self.last = None

    def __getattr__(self, name):
        def rec(*a, **k):
            self.last = (name, a, k)
            return _FakeInst()
        return rec


def _free(ap):
    n = 1
    for d in list(ap.shape)[1:]:
        n *= int(d)
    return n


class Sched:
    ENGS = {"pe": "tensor", "act": "scalar", "dve": "vector", "pool": "gpsimd", "sp": "sync"}

    def __init__(self):
        self.ops = []
        self.lw = {}
        self.rd = {}

    def add(self, eng, fn, reads=(), writes=(), grp=None):
        op = Op()
        op.eng, op.fn, op.grp, op.idx = eng, fn, grp, len(self.ops)
        op.needed = False
        deps = {}

        def dep(o, raw):
            if o is None or o is op:
                return
            p = deps.get(o.idx)
            deps[o.idx] = (o, raw or (p[1] if p else False))

        for r in reads:
            dep(self.lw.get(r), True)
        for w in writes:
            dep(self.lw.get(w), False)
            for o in self.rd.get(w, ()):
                dep(o, False)
        for r in reads:
            self.rd.setdefault(r, []).append(op)
        for w in writes:
            self.lw[w] = op
            self.rd[w] = []
        op.alldeps = list(deps.values())
        self.ops.append(op)
        return op

    def _cost(self, op):
        fe = _FakeE()
        op.fn(fe)
        name, a, k = fe.last
        if name == "matmul":
            n = _free(a[0])
            c = max(n, 64) * 0.47 + 6
            return c, c
        if name == "transpose":
            return 40.0, 40.0
        if name == "dma_start":
            src = k["in_"]
            nbytes = 4 * _free(src) * int(src.shape[0])
            issue = 1300.0 if op.eng == "pool" else 250.0
            return issue, nbytes
        if name == "nop":
            return 10.0, 10.0
        out = k.get("out", a[0] if a else None)
        n = _free(out)
        if op.eng == "act":
            c = 230 + n * 1.05
        elif op.eng == "pool":
            c = 150 + n * 2.7
        else:
            c = 90 + n * 1.15
        return c, c

    def schedule(self):
        import heapq
        ops = self.ops
        nops = len(ops)
        succ = [[] for _ in range(nops)]
        indeg = [0] * nops
        for op in ops:
            indeg[op.idx] = len(op.alldeps)
            for d, _ in op.alldeps:
                succ[d.idx].append(op.idx)
        done = [0.0] * nops
        ready_t = [0.0] * nops
        engs = list(self.ENGS)
        h_wait = {e: [] for e in engs}
        h_rdy = {e: [] for e in engs}
        free = {e: 0.0 for e in engs}
        order = {e: [] for e in engs}
        dma_free = 0.0
        for op in ops:
            op.busy, op.cost = self._cost(op)
            if indeg[op.idx] == 0:
                heapq.heappush(h_wait[op.eng], (0.0, op.idx))
        nsched = 0
        while nsched < nops:
            best = None
            for e in engs:
                t = free[e]
                hw, hr = h_wait[e], h_rdy[e]
                while hw and hw[0][0] <= t:
                    heapq.heappush(hr, heapq.heappop(hw)[1])
                if hr:
                    cand = (t, hr[0], e, True)
                elif hw:
                    cand = (hw[0][0], hw[0][1], e, False)
                else:
                    continue
                if best is None or cand[:2] < best[:2]:
                    best = cand
            assert best is not None, "scheduler deadlock"
            t, idx, e, from_rdy = best
            if from_rdy:
                heapq.heappop(h_rdy[e])
            else:
                heapq.heappop(h_wait[e])
            op = ops[idx]
            start = max(t, ready_t[idx])
            if op.grp is not None:
                free[e] = start + op.busy
                xfer = op.cost * 0.0031
                dma_free = max(dma_free, start + 1000.0) + xfer
                done[idx] = dma_free + 1000.0
            else:
                free[e] = start + op.busy
                done[idx] = start + op.cost + 60.0
            op.pos = len(order[e])
            order[e].append(op)
            nsched += 1
            for sidx in succ[idx]:
                indeg[sidx] -= 1
                if done[idx] > ready_t[sidx]:
                    ready_t[sidx] = done[idx]
                if indeg[sidx] == 0:
                    so = ops[sidx]
                    heapq.heappush(h_wait[so.eng], (ready_t[sidx], sidx))
        self.order = order
        self.est_ns = max(done) if done else 0.0

    def emit(self, nc, es):
        self.schedule()
        engs = self.ENGS
        order = self.order
        for op in self.ops:
            best = {}
            dmad = {}
            for d, raw in op.alldeps:
                if d.grp is not None:
                    p = dmad.get(d.grp)
                    if p is None or d.pos > p.pos:
                        dmad[d.grp] = d
                    continue
                if op.grp is None and d.eng == op.eng:
                    if op.eng == "pe" or not raw:
                        continue
                p = best.get(d.eng)
                if p is None or d.pos > p.pos:
                    best[d.eng] = d
            op.deps = list(best.values()) + list(dmad.values())
            for d in op.deps:
                d.needed = True
        esem = {e: es.enter_context(nc.semaphore("s_" + e)) for e in engs}
        gsem = {}
        gcount = {}
        cnt = {e: 0 for e in engs}
        for e in engs:
            for op in order[e]:
                if op.grp is not None:
                    if op.grp not in gsem:
                        gsem[op.grp] = es.enter_context(nc.semaphore("g_" + op.grp))
                        gcount[op.grp] = 0
                    gcount[op.grp] += 1
                    op.val = 16 * gcount[op.grp]
                elif op.needed:
                    cnt[e] += 1
                    op.cnt = cnt[e]
        assert max(cnt.values()) < 60000, cnt
        block = es.enter_context(nc.Block())

        def make(e):
            def body(E):
                waited = {}
                for op in order[e]:
                    for d in op.deps:
                        if d.grp is not None:
                            sem, val = gsem[d.grp], d.val
                        else:
                            sem, val = esem[d.eng], d.cnt
                        key = id(sem)
                        if waited.get(key, 0) < val:
                            E.wait_ge(sem, val)
                            waited[key] = val
                    inst = op.fn(E)
                    if op.grp is not None:
                        inst.then_inc(gsem[op.grp], 16)
                    elif op.needed:
                        inst.then_inc(esem[e], 1)
            return body

        for e, attr in engs.items():
            getattr(block, attr)(make(e))


def build(T, depth, mixers=(0, 1, 2, 3)):
    NT = T + NSAMP
    nc = bass.Bass("TRN2", target_bir_lowering=False)
    dr = lambda n, s, k="ExternalInput": nc.dram_tensor(n, list(s), F32, kind=k).ap()
    d_x = dr("xT", [128, KC, NT])
    d_c = dr("cT", [128, KC, 17])
    d_st3 = dr("st3", [128, KC, NS, 2])
    d_st31 = dr("st31", [128, KC, NS, 30])
    d_stp = dr("stp", [128, KC, NS, 15])
    d_pv = dr("pvec", [128, NPV])
    d_ada = dr("adaw", [DEPTH * 36, 128, 2048])
    d_win = dr("win", [DEPTH * 2 * NJ, 128, 2048])
    d_wout = dr("wout", [DEPTH * 2 * DFF, D])
    d_mw = dr("mixw", [NMW, 128, 2048])
    d_ws = dr("wsT", [128, 2, 4, 128])
    d_bs = dr("bsrow", [1, 2, 512])
    d_dwg = dr("dwg", [128, 2048])
    d_cst = dr("consts", [128, 320])
    o_y = dr("yT", [128, KC, NT], "ExternalOutput")
    o_v = dr("vch", [128, KC, 256], "ExternalOutput")
    o_c3p = dr("oc3p", [128, KC, 2], "ExternalOutput")
    o_c3s = dr("oc3s", [128, KC, NS * 10], "ExternalOutput")
    o_c31p = dr("oc31p", [128, KC, 30], "ExternalOutput")
    o_c31a = dr("oc31a", [128, KC, NS, 30], "ExternalOutput")
    o_c31b = dr("oc31b", [128, KC, NSAMP], "ExternalOutput")
    o_plp = dr("oplp", [128, KC, 15], "ExternalOutput")
    o_pls = dr("opls", [128, KC, NS * 23], "ExternalOutput")

    S = Sched()
    es = contextlib.ExitStack()
    sb = lambda n, s, dt=F32: es.enter_context(nc.sbuf_tensor(n, list(s), dt))
    gp = [(t0, min(512, T - t0), False) for t0 in range(0, T, 512)]
    groups = gp + [(T, NSAMP, True)]
    nh = max(1, len(gp) // 2)
    SGS = [list(range(0, nh)), list(range(nh, len(groups)))]
    sgmax = max(sum(groups[g][1] for g in sg) for sg in SGS)

    x = sb("x", [128, KC, NT])
    hbf = sb("hbf", [128, KC, sgmax], BF16)
    SCR = 16384
    scr = sb("scr", [128, SCR])
    wring = sb("wring", [128, 3, KC, 256], BF16)
    lnzb = sb("lnzb", [128, KC, 512], BF16)
    lnzq = sb("lnzq", [128, KC, 512], BF16)
    lnm = sb("lnm", [128, 512])
    lnt = sb("lnt", [128, 512])
    modT = sb("modT", [128, 2, 72, 17])
    gb = sb("gb", [128, KC, 17])
    g2 = sb("g2", [128, KC, 17])
    pv = sb("pv", [128, NPV])
    smp = sb("smp", [128, 2, 128])
    cst = sb("cst", [128, 320])
    ctmp = sb("ctmp", [128, KC, 17])
    scT = sb("scT", [128, KC, 17], BF16)
    identb = sb("identb", [128, 128], BF16)
    onesM = sb("onesM", [128, 128], BF16)
    onesr = sb("onesr", [1, 128], BF16)
    fdum = sb("fdum", [128, 1])
    ps = [es.enter_context(nc.psum_tensor(f"ps{i}", [128, 512], F32)) for i in range(8)]

    def carve(off, shape, dt=F32):
        n = int(np.prod(shape))
        nf = n if dt == F32 else (n + 1) // 2
        assert off + nf <= SCR, (off, nf)
        a = scr[:, off:off + nf]
        if dt != F32:
            a = a.bitcast(BF16)
        if len(shape) == 1:
            return a, off + nf
        names = "abcd"[:len(shape)]
        pat = "p (" + " ".join(names) + ") -> p " + " ".join(names)
        kw = {names[i]: shape[i] for i in range(1, len(shape))}
        return a.rearrange(pat, **kw), off + nf

    MX = ("mxgen",)
    dmaq = [0]

    def dma(eng, out, in_, reads, writes, grp):
        return S.add(eng, lambda E: E.dma_start(out=out, in_=in_), reads, writes, grp=grp)

    fence_on = [True]

    def fence():
        if fence_on[0]:
            S.add("sp", lambda E: E.nop(), reads=[], writes=[MX])

    dma("sp", pv[:], d_pv, [], ["pv"], "i_pv")
    dma("sp", cst[:], d_cst, [], ["cst"], "i_cst")
    dma("sp", ctmp[:], d_c, [], ["ctmp"], "i_c")
    for gi, (t0, n, smpl) in enumerate(groups):
        dma("sp", x[:, :, t0:t0 + n], d_x[:, :, t0:t0 + n], [], [("x", gi, m) for m in range(KC)], f"i_x{gi}")
    S.add("act", lambda E: E.activation(out=scT[:], in_=ctmp[:], func=AF.Silu), ["ctmp"], ["scT"])
    S.add("dve", lambda E: E.tensor_copy(out=identb[:], in_=cst[:, 0:128]), ["cst"], ["identb"])
    S.add("dve", lambda E: E.memset(onesM[:], 1.0 / 1024.0), [], ["onesM"])
    S.add("dve", lambda E: E.memset(onesr[:], 1.0), [], ["onesr"])
    fence()

    wcnt = [0]

    def wload(src_tile):
        i = wcnt[0]
        wcnt[0] += 1
        s = i % 3
        key = ("wring", s)
        dma("pool", wring[:, s].rearrange("p k c -> p (k c)"), src_tile, [], [key], f"w{s}")
        return wring[:, s], key

    def xs(gi):
        return [("x", gi, m) for m in range(KC)]

    ada_q = []

    def ada_enqueue(l):
        mb = l % 2
        for nt in range(36):
            def step(nt=nt):
                s_ = nt // 12
                mk = ("modT", mb, s_)
                slot, wk = wload(d_ada[l * 36 + nt])
                bank = 4 + (ecnt[0] % 2)
                ecnt[0] += 1
                n0 = nt * 2
                for half in range(2):
                    for kc in range(KC):
                        S.add("pe", lambda E, kc=kc, half=half, slot=slot, bank=bank: E.matmul(
                            ps[bank][:, half * 17:half * 17 + 17], lhsT=slot[:, kc, half * 128:(half + 1) * 128], rhs=scT[:, kc, :],
                            start=(kc == 0), stop=(kc == KC - 1)), [wk, "scT"], [("ps", bank)])
                S.add("dve", lambda E: E.tensor_tensor(
                    out=modT[:, mb, n0:n0 + 2, :], in0=ps[bank][:, 0:34].rearrange("p (a b) -> p a b", b=17),
                    in1=pv[:, PV["ada_b"] + l * 72 + n0:PV["ada_b"] + l * 72 + n0 + 2].unsqueeze(2).broadcast_to([128, 2, 17]),
                    op=ALU.add), [("ps", bank), "pv"], [mk])
                if nt % 12 == 11:
                    n1 = (s_ * 3 + 1) * 8
                    n2 = (s_ * 3 + 2) * 8
                    c = (0.5 if s_ != 1 else 1.0) / ALPHA
                    S.add("dve", lambda E: E.tensor_scalar(out=modT[:, mb, n1:n1 + 8, :], in0=modT[:, mb, n1:n1 + 8, :],
                                                           scalar1=1.0, scalar2=None, op0=ALU.add), [mk], [mk])
                    S.add("dve", lambda E: E.tensor_scalar(out=modT[:, mb, n2:n2 + 8, :], in0=modT[:, mb, n2:n2 + 8, :],
                                                           scalar1=1.0, scalar2=c, op0=ALU.add, op1=ALU.mult), [mk], [mk])
            ada_q.append(step)

    def ada_pump(k):
        for _ in range(min(k, len(ada_q))):
            ada_q.pop(0)()

    def mod_ap(l, s, j):
        n = (s * 3 + j) * 8
        return modT[:, l % 2, n:n + 8, :]

    def sg_off(sg, gi):
        return sum(groups[g][1] for g in sg if g < gi)

    def modulate(l, s, sg):
        sh, sc = mod_ap(l, s, 0), mod_ap(l, s, 1)
        mk = ("modT", l % 2, s)
        for gi in sg:
            t0, n, smpl = groups[gi]
            h0 = sg_off(sg, gi)
            if not smpl:
                for kc in range(KC):
                    S.add("act", lambda E, kc=kc, t0=t0, n=n, h0=h0: E.activation(
                        out=hbf[:, kc, h0:h0 + n], in_=x[:, kc, t0:t0 + n], func=AF.Identity,
                        scale=sc[:, kc, 0:1], bias=sh[:, kc, 0:1]), [("x", gi, kc), mk], [("hbf", sg.index(gi))])
            else:
                for kc in range(KC):
                    b = kc % 2
                    S.add("dve", lambda E, kc=kc, t0=t0, b=b: E.tensor_tensor(
                        out=smp[:, b, :].rearrange("p (s t) -> p s t", t=SL),
                        in0=x[:, kc, t0:t0 + NSAMP].rearrange("p (s t) -> p s t", t=SL),
                        in1=sc[:, kc, 1:17].unsqueeze(2).broadcast_to([128, NS, SL]), op=ALU.mult),
                        [("x", gi, kc), mk], [("smp", b)])
                    S.add("dve", lambda E, kc=kc, h0=h0, b=b: E.tensor_tensor(
                        out=hbf[:, kc, h0:h0 + NSAMP].rearrange("p (s t) -> p s t", t=SL),
                        in0=smp[:, b, :].rearrange("p (s t) -> p s t", t=SL),
                        in1=sh[:, kc, 1:17].unsqueeze(2).broadcast_to([128, NS, SL]), op=ALU.add),
                        [("smp", b), mk], [("hbf", sg.index(gi))])

    def epilogue(gi, m, bank, gate, gk, extra_reads=()):
        t0, n, smpl = groups[gi]
        if not smpl:
            S.add("dve", lambda E: E.scalar_tensor_tensor(
                out=x[:, m, t0:t0 + n], in0=ps[bank][:, 0:n], scalar=gate[:, m, 0:1], in1=x[:, m, t0:t0 + n],
                op0=ALU.mult, op1=ALU.add), [("ps", bank), ("x", gi, m), gk, *extra_reads], [("x", gi, m)])
        else:
            b = m % 2
            S.add("dve", lambda E: E.tensor_tensor(
                out=smp[:, b, :].rearrange("p (s t) -> p s t", t=SL),
                in0=ps[bank][:, 0:n].rearrange("p (s t) -> p s t", t=SL),
                in1=gate[:, m, 1:17].unsqueeze(2).broadcast_to([128, NS, SL]), op=ALU.mult),
                [("ps", bank), gk, *extra_reads], [("smp", b)])
            S.add("dve", lambda E: E.tensor_tensor(
                out=x[:, m, t0:t0 + n], in0=smp[:, b, :], in1=x[:, m, t0:t0 + n], op=ALU.add),
                [("smp", b), ("x", gi, m)], [("x", gi, m)])

    def prebias(gi, bias, bk):
        t0, n, smpl = groups[gi]
        for m in range(KC):
            if not smpl:
                S.add("act", lambda E, m=m: E.activation(out=x[:, m, t0:t0 + n], in_=x[:, m, t0:t0 + n],
                                                         func=AF.Identity, bias=bias[:, m, 0:1], scale=1.0),
                      [("x", gi, m), bk], [("x", gi, m)])
            else:
                S.add("dve", lambda E, m=m: E.tensor_tensor(
                    out=x[:, m, t0:t0 + n].rearrange("p (s t) -> p s t", t=SL),
                    in0=x[:, m, t0:t0 + n].rearrange("p (s t) -> p s t", t=SL),
                    in1=bias[:, m, 1:17].unsqueeze(2).broadcast_to([128, NS, SL]), op=ALU.add),
                    [("x", gi, m), bk], [("x", gi, m)])

    def ln_norm(view, n, rkeys, eps, extra=()):
        wk_ = list(rkeys)
        allk = list(rkeys) + list(extra)
        halves = [(0, n)] if n <= 256 else [(0, n // 2), (n // 2, n - n // 2)]
        for hi, (c0, nh) in enumerate(halves):
            bank = 6 + hi
            pk = ("ps", bank)
            vw = view[:, :, c0:c0 + nh]
            zb, zq = lnzb[:, :, c0:c0 + nh], lnzq[:, :, c0:c0 + nh]
            kzb, kzq, klm, klt = ("lnzb", hi), ("lnzq", hi), ("lnm", hi), ("lnt", hi)
            mean_ps, msq_ps = ps[bank][:, 0:nh], ps[bank][:, 256:256 + nh]
            lm, lt = lnm[:, c0:c0 + nh], lnt[:, c0:c0 + nh]
            S.add("act", lambda E, zb=zb, vw=vw: E.activation(out=zb, in_=vw, func=AF.Copy), allk, [kzb])
            S.add("act", lambda E, zq=zq, vw=vw: E.activation(out=zq, in_=vw, func=AF.Square), allk, [kzq])
            for kc in range(KC):
                S.add("pe", lambda E, kc=kc, mean_ps=mean_ps, zb=zb: E.matmul(mean_ps, lhsT=onesM[:], rhs=zb[:, kc, :],
                                                                           start=(kc == 0), stop=(kc == KC - 1)), [kzb, "onesM"], [pk])
            for kc in range(KC):
                S.add("pe", lambda E, kc=kc, msq_ps=msq_ps, zq=zq: E.matmul(msq_ps, lhsT=onesM[:], rhs=zq[:, kc, :],
                                                                         start=(kc == 0), stop=(kc == KC - 1)), [kzq, "onesM"], [pk])
            S.add("act", lambda E, lm=lm, mean_ps=mean_ps: E.activation(out=lm, in_=mean_ps, func=AF.Copy), [pk], [klm])
            S.add("dve", lambda E, lt=lt, lm=lm, mean_ps=mean_ps: E.tensor_tensor(out=lt, in0=lm, in1=mean_ps, op=ALU.mult),
                  [klm, pk], [klt])
            S.add("dve", lambda E, lt=lt, msq_ps=msq_ps: E.tensor_tensor(out=lt, in0=msq_ps, in1=lt, op=ALU.subtract),
                  [klt, pk], [klt])
            S.add("act", lambda E, lt=lt: E.activation(out=lt, in_=lt, func=AF.Sqrt, bias=epsc(eps), scale=1.0),
                  [klt, "epsc"], [klt])
            S.add("dve", lambda E, lt=lt, mean_ps=mean_ps: E.reciprocal(out=mean_ps, in_=lt), [klt], [pk])
            S.add("dve", lambda E, lm=lm, mean_ps=mean_ps, msq_ps=msq_ps: E.scalar_tensor_tensor(
                out=msq_ps, in0=lm, scalar=-1.0, in1=mean_ps, op0=ALU.mult, op1=ALU.mult), [klm, pk], [pk])
            for kc in range(KC):
                vkc = view[:, kc, c0:c0 + nh]
                S.add("dve", lambda E, vkc=vkc, mean_ps=mean_ps: E.tensor_tensor(out=vkc, in0=vkc, in1=mean_ps, op=ALU.mult),
                      [wk_[kc], pk] + list(extra), [wk_[kc]])
                S.add("dve", lambda E, vkc=vkc, msq_ps=msq_ps: E.tensor_tensor(out=vkc, in0=vkc, in1=msq_ps, op=ALU.add),
                      [wk_[kc], pk] + list(extra), [wk_[kc]])

    epst = sb("epst", [128, 2])

    def epsc(eps):
        return epst[:, 0:1] if eps == EPS_MAIN else epst[:, 1:2]

    S.add("dve", lambda E: E.memset(epst[:, 0:1], EPS_MAIN), [], ["epsc"])
    S.add("dve", lambda E: E.memset(epst[:, 1:2], LN_EPS), [], ["epsc"])

    def ln_main(l, s, gi):
        t0, n, smpl = groups[gi]
        view = x[:, :, t0:t0 + n]
        ln_norm(view, n, xs(gi), EPS_MAIN)
        go = PV["ln_g"] + (l * 3 + s) * 8
        bo = PV["ln_b"] + (l * 3 + s) * 8
        for kc in range(KC):
            S.add("act", lambda E, kc=kc: E.activation(out=x[:, kc, t0:t0 + n], in_=x[:, kc, t0:t0 + n], func=AF.Identity,
                                                       scale=pv[:, go + kc:go + kc + 1], bias=pv[:, bo + kc:bo + kc + 1]),
                  [("x", gi, kc), "pv"], [("x", gi, kc)])

    o_hid = 0
    hid, o1 = carve(0, [KP, sgmax], BF16)
    woutb, o2 = carve(o1, [2, KP, D], BF16)
    sgt, o3 = carve(o2, [2, 512])
    pcnt = [0]
    ecnt = [0]
    wocnt = [0]

    def ffn(l, f, sg, after_last_phase1):
        s = 0 if f == 0 else 2
        gate = mod_ap(l, s, 2)
        gk = ("modT", l % 2, s)
        lf = l * 2 + f
        for pi, part in enumerate(PARTS):
            for jj, j in enumerate(part):
                slot, wk = wload(d_win[lf * NJ + j])
                for gi in sg:
                    t0, n, smpl = groups[gi]
                    h0 = sg_off(sg, gi)
                    pp = pcnt[0] % 2
                    pcnt[0] += 1
                    bg_, bu_ = 2 * pp, 2 * pp + 1
                    for half, bank in ((0, bg_), (1, bu_)):
                        for kc in range(KC):
                            S.add("pe", lambda E, kc=kc, half=half, bank=bank, slot=slot, h0=h0, n=n: E.matmul(
                                ps[bank][:, 0:n], lhsT=slot[:, kc, half * 128:(half + 1) * 128], rhs=hbf[:, kc, h0:h0 + n],
                                start=(kc == 0), stop=(kc == KC - 1)), [wk, ("hbf", sg.index(gi)), MX], [("ps", bank)])
                    S.add("act", lambda E, pp=pp, bg_=bg_, n=n: E.activation(out=sgt[:, pp, 0:n], in_=ps[bg_][:, 0:n], func=AF.Silu),
                          [("ps", bg_), MX], [("sgt", pp)])
                    S.add("dve", lambda E, pp=pp, bu_=bu_, n=n, jj=jj, h0=h0: E.tensor_tensor(
                        out=hid[:, jj, h0:h0 + n], in0=sgt[:, pp, 0:n], in1=ps[bu_][:, 0:n], op=ALU.mult),
                        [("sgt", pp), ("ps", bu_), MX], [("hid", jj, sg.index(gi))])
            ada_pump(8)
            if pi == len(PARTS) - 1 and after_last_phase1 is not None:
                after_last_phase1()
            wb = wocnt[0] % 2
            wocnt[0] += 1
            wok = ("woutb", wb)
            r0 = lf * DFF + part[0] * 128
            kp = len(part)
            dma("pool", woutb[:, wb, 0:kp, :], d_wout[r0:r0 + kp * 128, :].rearrange("(k p) n -> p k n", p=128),
                [MX], [wok], f"wo{wb}")
            for gi in sg:
                t0, n, smpl = groups[gi]
                h0 = sg_off(sg, gi)
                for m in range(KC):
                    bank = 4 + (ecnt[0] % 2)
                    ecnt[0] += 1
                    for jj in range(kp):
                        S.add("pe", lambda E, jj=jj, m=m, bank=bank, wb=wb, h0=h0, n=n, kp=kp: E.matmul(
                            ps[bank][:, 0:n], lhsT=woutb[:, wb, jj, m * 128:(m + 1) * 128], rhs=hid[:, jj, h0:h0 + n],
                            start=(jj == 0), stop=(jj == kp - 1)), [wok, ("hid", jj, sg.index(gi)), MX], [("ps", bank)])
                    epilogue(gi, m, bank, gate, gk)
                if pi == len(PARTS) - 1:
                    if gi == sg[-1]:
                        fence()
                    ln_main(l, s, gi)

    def inproj(wname, tiles, gi, sg, cb):
        t0, n, smpl = groups[gi]
        h0 = sg_off(sg, gi)
        base = MW[wname][0]
        for ti in tiles:
            slot, wk = wload(d_mw[base + ti])
            for half in range(2):
                bank = pcnt[0] % 4
                pcnt[0] += 1
                for kc in range(KC):
                    S.add("pe", lambda E, kc=kc, half=half, bank=bank, slot=slot: E.matmul(
                        ps[bank][:, 0:n], lhsT=slot[:, kc, half * 128:(half + 1) * 128], rhs=hbf[:, kc, h0:h0 + n],
                        start=(kc == 0), stop=(kc == KC - 1)), [wk, ("hbf", sg.index(gi))], [("ps", bank)])
                cb(ti * 2 + half, bank)

    def outproj(wname, gi, src, srck, gate, gk):
        t0, n, smpl = groups[gi]
        base = MW[wname][0]
        slots = [wload(d_mw[base + ti]) for ti in range(2)]
        for ti in range(4):
            if ti >= 2:
                slots.append(wload(d_mw[base + ti]))
            slot, wk = slots[ti]
            for half in range(2):
                m = ti * 2 + half
                bank = 4 + (ecnt[0] % 2)
                ecnt[0] += 1
                for kc in range(KC):
                    S.add("pe", lambda E, kc=kc, half=half, bank=bank, slot=slot: E.matmul(
                        ps[bank][:, 0:n], lhsT=slot[:, kc, half * 128:(half + 1) * 128], rhs=src(kc),
                        start=(kc == 0), stop=(kc == KC - 1)), [wk, MX] + list(srck), [("ps", bank)])
                epilogue(gi, m, bank, gate, gk)

    def pvc(name, c):
        o = PV[name] + c
        return pv[:, o:o + 1]

    def mixer_gmlp(l, sg, after_in):
        gate = mod_ap(l, 1, 2)
        gk = ("modT", l % 2, 1)
        ub, o = carve(0, [KC, 512], BF16)
        vf, o = carve(o, [KC, 512])
        vb, o = carve(o, [KC, 512], BF16)
        vtm, o = carve(o, [1024], BF16)
        wsb, o = carve(o, [2, 4, 128], BF16)
        wsf, o = carve(o, [2, 4, 128])
        bsb, o = carve(o, [2, 512], BF16)
        bsf, o = carve(o, [2, 512])
        if sg is SGS[0]:
            pass
        dma("sp", wsf, d_ws, [MX], ["wsf"], "i_ws")
        dma("sp", bsf[0:1], d_bs, [MX], ["bsf"], "i_bs")
        for v in range(2):
            S.add("dve", lambda E, v=v: E.tensor_tensor(out=wsb[:, v], in0=wsf[:, v],
                                                        in1=cst[:, 128:256].unsqueeze(1).broadcast_to([128, 4, 128]), op=ALU.mult),
                  ["wsf", "cst", MX], ["wsb"])
        S.add("dve", lambda E: E.tensor_copy(out=bsb[0:1], in_=bsf[0:1]), ["bsf", MX], ["bsb"])
        def _grp(gidx, gi):
            t0, n, smpl = groups[gi]

            def cb(ch, bank, gi=gi, n=n):
                bcol = pvc("a_b_in", ch)
                if ch < 8:
                    dst, key = ub[:, ch, 0:n], ("ub", ch)
                else:
                    dst, key = vf[:, ch - 8, 0:n], ("vf", ch - 8)
                S.add("act", lambda E: E.activation(out=dst, in_=ps[bank][:, 0:n], func=AF.Gelu_apprx_tanh, bias=bcol, scale=1.0),
                      [("ps", bank), "pv", MX], [key])

            inproj("a_in", [4, 5, 6, 7, 0, 1, 2, 3], gi, sg, cb)
            if gidx == len(sg) - 1 and after_in is not None:
                after_in()
            vk = [("vf", c) for c in range(KC)]
            ln_norm(vf[:, :, 0:n], n, vk, LN_EPS, [MX])
            need_v = smpl or (t0 + n == T)
            for c in range(KC):
                if need_v:
                    S.add("act", lambda E, c=c: E.activation(out=vf[:, c, 0:n], in_=vf[:, c, 0:n], func=AF.Identity,
                                                             scale=pvc("a_ln_g", c), bias=pvc("a_ln_b", c)),
                          [("vf", c), "pv", MX], [("vf", c)])
                    S.add("dve", lambda E, c=c: E.tensor_copy(out=vb[:, c, 0:n], in_=vf[:, c, 0:n]), [("vf", c), MX], [("vb", c)])
                else:
                    S.add("act", lambda E, c=c: E.activation(out=vb[:, c, 0:n], in_=vf[:, c, 0:n], func=AF.Identity,
                                                             scale=pvc("a_ln_g", c), bias=pvc("a_ln_b", c)),
                          [("vf", c), "pv", MX], [("vb", c)])
            if smpl:
                dma("sp", o_v[:, :, 128:256], vf[:, :, 0:128], vk + [MX], [], "o_v1")
            elif t0 + n == T:
                dma("sp", o_v[:, :, 0:128], vf[:, :, n - 128:n], vk + [MX], [], "o_v0")
            vi = 1 if smpl else 0
            for sub in range(n // 128):
                c0 = sub * 128
                psT = ps[0][:].bitcast(BF16)
                for c in range(KC):
                    S.add("pe", lambda E, c=c, c0=c0: E.transpose(psT[:, c * 128:(c + 1) * 128], vb[:, c, c0:c0 + 128], identb[:]),
                          [("vb", c), "identb", MX], [("ps", 0)])
                S.add("act", lambda E: E.activation(out=vtm, in_=psT, func=AF.Copy), [("ps", 0), MX], ["vtm"])
                for hf in range(2):
                    bank = 1 + hf
                    for cc in range(4):
                        c = hf * 4 + cc
                        grp = c // 2
                        S.add("pe", lambda E, c=c, cc=cc, grp=grp, bank=bank: E.matmul(
                            ps[bank][:, cc * 128:(cc + 1) * 128], lhsT=vtm[:, c * 128:(c + 1) * 128],
                            rhs=wsb[:, vi, grp, :], start=True, stop=False), ["vtm", "wsb", MX], [("ps", bank)])
                        S.add("pe", lambda E, cc=cc, grp=grp, bank=bank: E.matmul(
                            ps[bank][:, cc * 128:(cc + 1) * 128], lhsT=onesr[:, :], rhs=bsb[0:1, vi, grp * 128:(grp + 1) * 128],
                            start=False, stop=True), ["onesr", "bsb", MX], [("ps", bank)])
                    S.add("dve", lambda E, hf=hf, bank=bank, c0=c0: E.tensor_tensor(
                        out=ub[:, hf * 4:hf * 4 + 4, c0:c0 + 128], in0=ub[:, hf * 4:hf * 4 + 4, c0:c0 + 128],
                        in1=ps[bank][:, :].rearrange("p (a b) -> p a b", b=128), op=ALU.mult),
                        [("ub", hf * 4 + i) for i in range(4)] + [("ps", bank), MX], [("ub", hf * 4 + i) for i in range(4)])
            outproj("a_out", gi, lambda kc, n=n: ub[:, kc, 0:n], [("ub", c) for c in range(KC)], gate, gk)
            if gi == sg[-1]:
                fence()
            ln_main(l, 1, gi)

        for gidx_, gi_ in enumerate(sg):
            _grp(gidx_, gi_)

    tail3 = sb("tail3", [128, KC, 2])
    tail31 = sb("tail31", [128, KC, 30], BF16)

    def mixer_sconv(l, sg, after_in):
        gate = mod_ap(l, 1, 2)
        gk = ("modT", l % 2, 1)
        cg, o = carve(0, [KC, 512])
        Pp, o = carve(o, [KC, 514])
        yb, o = carve(o, [KC, 512], BF16)
        s3t, o = carve(o, [KC, NS, 2])
        Ps = Pp[:, :, 0:NS * 10].rearrange("p k (s t) -> p k s t", t=10)
        def _grp(gidx, gi):
            t0, n, smpl = groups[gi]
            Pk = [("P3", c) for c in range(KC)]
            if smpl:
                dma("sp", s3t, d_st3, [MX], ["s3t"], "i_st3")
                S.add("dve", lambda E: E.tensor_copy(out=Ps[:, :, :, 0:2], in_=s3t), ["s3t", MX], Pk)
            elif t0 == 0:
                S.add("dve", lambda E: E.memset(Pp[:, :, 0:2], 0.0), [MX], Pk)
            else:
                S.add("dve", lambda E: E.tensor_copy(out=Pp[:, :, 0:2], in_=tail3[:]), ["tail3", MX], Pk)

            def cb(ch, bank, gi=gi, n=n, smpl=smpl):
                if 8 <= ch < 16:
                    c = ch - 8
                    S.add("act", lambda E: E.activation(out=cg[:, c, 0:n], in_=ps[bank][:, 0:n], func=AF.Copy),
                          [("ps", bank), MX], [("cg", c)])
                elif ch >= 16:
                    c = ch - 16
                    if smpl:
                        S.add("dve", lambda E: E.tensor_tensor(
                            out=Ps[:, c, :, 2:10], in0=cg[:, c, 0:n].rearrange("p (s t) -> p s t", t=SL),
                            in1=ps[bank][:, 0:n].rearrange("p (s t) -> p s t", t=SL), op=ALU.mult),
                            [("cg", c), ("ps", bank), MX], [("P3", c)])
                        w = lambda k: Ps[:, c, :, k:k + SL]
                        cv = cg[:, c, 0:n].rearrange("p (s t) -> p s t", t=SL)
                    else:
                        S.add("dve", lambda E: E.tensor_tensor(out=Pp[:, c, 2:2 + n], in0=cg[:, c, 0:n], in1=ps[bank][:, 0:n],
                                                               op=ALU.mult), [("cg", c), ("ps", bank), MX], [("P3", c)])
                        w = lambda k: Pp[:, c, k:k + n]
                        cv = cg[:, c, 0:n]
                    wc = lambda k: pv[:, PV["b_conv"] + k * 8 + c:PV["b_conv"] + k * 8 + c + 1]
                    S.add("dve", lambda E: E.tensor_scalar(out=cv, in0=w(2), scalar1=wc(2), scalar2=None, op0=ALU.mult),
                          [("P3", c), "pv", MX], [("cg", c)])
                    S.add("dve", lambda E: E.scalar_tensor_tensor(out=cv, in0=w(1), scalar=wc(1), in1=cv, op0=ALU.mult, op1=ALU.add),
                          [("P3", c), ("cg", c), "pv", MX], [("cg", c)])
                    S.add("dve", lambda E: E.scalar_tensor_tensor(out=cv, in0=w(0), scalar=wc(0), in1=cv, op0=ALU.mult, op1=ALU.add),
                          [("P3", c), ("cg", c), "pv", MX], [("cg", c)])
                else:
                    c = ch
                    S.add("dve", lambda E: E.tensor_tensor(out=yb[:, c, 0:n], in0=cg[:, c, 0:n], in1=ps[bank][:, 0:n], op=ALU.mult),
                          [("cg", c), ("ps", bank), MX], [("yb", c)])

            inproj("b_in", list(range(4, 12)) + list(range(0, 4)), gi, sg, cb)
            if gidx == len(sg) - 1 and after_in is not None:
                after_in()
            if smpl:
                dma("sp", o_c3s, Pp[:, :, 0:NS * 10], Pk + [MX], [], "o_c3s")
            elif t0 + n == T:
                dma("sp", o_c3p, Pp[:, :, n:n + 2], Pk + [MX], [], "o_c3p")
            else:
                S.add("act", lambda E, n=n: E.activation(out=tail3[:], in_=Pp[:, :, n:n + 2], func=AF.Copy), Pk + [MX], ["tail3"])
            outproj("b_out", gi, lambda kc, n=n: yb[:, kc, 0:n], [("yb", c) for c in range(KC)], gate, gk)
            if gi == sg[-1]:
                fence()
            ln_main(l, 1, gi)

        for gidx_, gi_ in enumerate(sg):
            _grp(gidx_, gi_)

    def mixer_conf(l, sg, after_in):
        gate = mod_ap(l, 1, 2)
        gk = ("modT", l % 2, 1)
        sgb, o = carve(0, [KC, 512])
        Gp, o = carve(o, [KC, 608], BF16)
        dg2, o = carve(o, [2, 31, 128], BF16)
        hb2, o = carve(o, [KC, 512], BF16)
        stf, o = carve(o, [KC, NS, 30])
        Gs = Gp[:, :, 0:NS * 38].rearrange("p k (s t) -> p k s t", t=38)
        S.add("dve", lambda E: E.tensor_tensor(out=gb[:], in0=gate, in1=pv[:, PV["c_b_pw2"]:PV["c_b_pw2"] + 8].unsqueeze(2).broadcast_to([128, KC, 17]),
                                               op=ALU.mult), [gk, "pv"], ["gb"])
        def _grp(gidx, gi):
            t0, n, smpl = groups[gi]
            Gk = [("G", c) for c in range(KC)]
            if smpl:
                dma("sp", stf, d_st31, [MX], ["stf"], "i_st31")
                S.add("dve", lambda E: E.tensor_copy(out=Gs[:, :, :, 0:30], in_=stf), ["stf", MX], Gk)
            elif t0 == 0:
                S.add("dve", lambda E: E.memset(Gp[:, :, 0:30], 0.0), [MX], Gk)
            else:
                S.add("dve", lambda E: E.tensor_copy(out=Gp[:, :, 0:30], in_=tail31[:]), ["tail31", MX], Gk)

            def cb(ch, bank, gi=gi, n=n, smpl=smpl):
                if ch >= 8:
                    c = ch - 8
                    S.add("act", lambda E: E.activation(out=sgb[:, c, 0:n], in_=ps[bank][:, 0:n], func=AF.Sigmoid,
                                                        bias=pvc("c_b_pw1", ch), scale=1.0), [("ps", bank), "pv", MX], [("sgb", c)])
                else:
                    c = ch
                    S.add("dve", lambda E: E.scalar_tensor_tensor(out=sgb[:, c, 0:n], in0=ps[bank][:, 0:n], scalar=pvc("c_b_pw1", c),
                                                                  in1=sgb[:, c, 0:n], op0=ALU.add, op1=ALU.mult),
                          [("ps", bank), ("sgb", c), "pv", MX], [("sgb", c)])
                    if smpl:
                        S.add("act", lambda E: E.activation(out=Gs[:, c, :, 30:38], in_=sgb[:, c, 0:n].rearrange("p (s t) -> p s t", t=SL),
                                                            func=AF.Copy), [("sgb", c), MX], [("G", c)])
                    else:
                        S.add("act", lambda E: E.activation(out=Gp[:, c, 30:30 + n], in_=sgb[:, c, 0:n], func=AF.Copy),
                              [("sgb", c), MX], [("G", c)])

            inproj("c_pw1", list(range(4, 8)) + list(range(0, 4)), gi, sg, cb)
            if gidx == len(sg) - 1 and after_in is not None:
                after_in()
            sk = [("sgb", c) for c in range(KC)]
            if smpl:
                dma("sp", o_c31a, stf, ["stf", MX], [], "o_c31a")
                dma("sp", o_c31b, sgb[:, :, 0:n], sk + [MX], [], "o_c31b")
            elif t0 + n == T:
                dma("sp", o_c31p, sgb[:, :, n - 30:n], sk + [MX], [], "o_c31p")
            else:
                S.add("act", lambda E, n=n: E.activation(out=tail31[:], in_=Gp[:, :, n:n + 30], func=AF.Copy), Gk + [MX], ["tail31"])
            for c in range(KC):
                dg = dg2[:, c % 2]
                dgk = ("dg", c % 2)
                S.add("dve", lambda E, c=c, dg=dg: E.tensor_tensor(
                    out=dg, in0=identb[:].unsqueeze(1).broadcast_to([128, 31, 128]),
                    in1=pv[:, PV["c_dw"] + c * 31:PV["c_dw"] + c * 31 + 31].unsqueeze(2).broadcast_to([128, 31, 128]),
                    op=ALU.mult), ["identb", "pv", MX], [dgk])
                bank = pcnt[0] % 4
                pcnt[0] += 1
                for k in range(31):
                    rhs = Gs[:, c, :, k:k + SL] if smpl else Gp[:, c, k:k + n]
                    outp = ps[bank][:, 0:n].rearrange("p (s t) -> p s t", t=SL) if smpl else ps[bank][:, 0:n]
                    S.add("pe", lambda E, k=k, rhs=rhs, outp=outp, dg=dg: E.matmul(outp, lhsT=dg[:, k, :], rhs=rhs, start=(k == 0), stop=(k == 30)),
                          [dgk, ("G", c), MX], [("ps", bank)])
                S.add("act", lambda E, c=c, bank=bank: E.activation(out=sgb[:, c, 0:n], in_=ps[bank][:, 0:n], func=AF.Identity,
                                                                    bias=pvc("c_b_dw", c), scale=1.0), [("ps", bank), "pv", MX], [("sgb", c)])
            ln_norm(sgb[:, :, 0:n], n, sk, LN_EPS, [MX])
            for c in range(KC):
                S.add("act", lambda E, c=c: E.activation(out=hb2[:, c, 0:n], in_=sgb[:, c, 0:n], func=AF.Silu,
                                                         scale=pvc("c_ln_g", c), bias=pvc("c_ln_b", c)), [("sgb", c), "pv", MX], [("hb2", c)])
            prebias(gi, gb[:], "gb")
            outproj("c_pw2", gi, lambda kc, n=n: hb2[:, kc, 0:n], [("hb2", c) for c in range(KC)], gate, gk)
            if gi == sg[-1]:
                fence()
            ln_main(l, 1, gi)

        for gidx_, gi_ in enumerate(sg):
            _grp(gidx_, gi_)

    tailp = sb("tailp", [128, KC, 15])

    def mixer_pool(l, sg, after_in):
        gate = mod_ap(l, 1, 2)
        gk = ("modT", l % 2, 1)
        mk = gk
        sh, sc = mod_ap(l, 1, 0), mod_ap(l, 1, 1)
        Hh, o = carve(0, [KC, 528])
        Ab, o = carve(o, [2, 528])
        Ab_p, o = carve(o, [2, 528])
        db, o = carve(o, [KC, 512], BF16)
        wgb, o = carve(o, [4, 2, 256], BF16)
        spt, o = carve(o, [KC, NS, 15])
        Hs = Hh[:, :, 0:NS * 23].rearrange("p k (s t) -> p k s t", t=23)
        dma("pool", wgb.rearrange("p a b c -> p (a b c)"), d_dwg, [MX], ["wgb"], "i_dwg")
        S.add("dve", lambda E: E.tensor_tensor(out=g2[:], in0=gate, in1=pv[:, PV["d_scale"]:PV["d_scale"] + 8].unsqueeze(2).broadcast_to([128, KC, 17]),
                                               op=ALU.mult), [gk, "pv"], ["g2"])
        S.add("dve", lambda E: E.tensor_tensor(out=gb[:], in0=g2[:], in1=pv[:, PV["d_b_grp"]:PV["d_b_grp"] + 8].unsqueeze(2).broadcast_to([128, KC, 17]),
                                               op=ALU.mult), ["g2", "pv"], ["gb"])
        if after_in is not None:
            after_in()
        def _grp(gidx, gi):
            t0, n, smpl = groups[gi]
            Hk = [("H", c) for c in range(KC)]
            if smpl:
                dma("sp", spt, d_stp, [MX], ["spt"], "i_stp")
                S.add("dve", lambda E: E.tensor_copy(out=Hs[:, :, :, 0:15], in_=spt), ["spt", MX], Hk)
                for kc in range(KC):
                    b = kc % 2
                    S.add("dve", lambda E, kc=kc, b=b: E.tensor_tensor(
                        out=smp[:, b, :].rearrange("p (s t) -> p s t", t=SL), in0=x[:, kc, t0:t0 + n].rearrange("p (s t) -> p s t", t=SL),
                        in1=sc[:, kc, 1:17].unsqueeze(2).broadcast_to([128, NS, SL]), op=ALU.mult), [("x", gi, kc), mk], [("smp", b)])
                    S.add("dve", lambda E, kc=kc, b=b: E.tensor_tensor(
                        out=Hs[:, kc, :, 15:23], in0=smp[:, b, :].rearrange("p (s t) -> p s t", t=SL),
                        in1=sh[:, kc, 1:17].unsqueeze(2).broadcast_to([128, NS, SL]), op=ALU.add), [("smp", b), mk, MX], [("H", kc)])
            else:
                if t0 == 0:
                    S.add("dve", lambda E: E.memset(Hh[:, :, 0:15], 0.0), [MX], Hk)
                else:
                    S.add("dve", lambda E: E.tensor_copy(out=Hh[:, :, 0:15], in_=tailp[:]), ["tailp", MX], Hk)
                for kc in range(KC):
                    S.add("act", lambda E, kc=kc: E.activation(out=Hh[:, kc, 15:15 + n], in_=x[:, kc, t0:t0 + n], func=AF.Identity,
                                                               scale=sc[:, kc, 0:1], bias=sh[:, kc, 0:1]), [("x", gi, kc), mk, MX], [("H", kc)])
            if smpl:
                dma("sp", o_pls, Hh[:, :, 0:NS * 23], Hk + [MX], [], "o_pls")
            elif t0 + n == T:
                dma("sp", o_plp, Hh[:, :, n:n + 15], Hk + [MX], [], "o_plp")
            else:
                S.add("act", lambda E, n=n: E.activation(out=tailp[:], in_=Hh[:, :, n:n + 15], func=AF.Copy), Hk + [MX], ["tailp"])
            for kc in range(KC):
                wi = kc // 2
                w = POOL_W[wi]
                on_pool = kc >= 4
                AB = Ab_p if on_pool else Ab
                abn = "Abp" if on_pool else "Ab"
                aeng = "pool" if on_pool else "dve"
                if smpl:
                    L = 23
                    nn = SL
                    Hv = lambda a, b_, kc=kc: Hs[:, kc, :, a:b_]
                    Av = lambda i, a, b_, AB=AB: AB[:, i, 0:NS * 23].rearrange("p (s t) -> p s t", t=23)[:, :, a:b_]
                    dv = db[:, kc, 0:n].rearrange("p (s t) -> p s t", t=SL)
                else:
                    L = 15 + n
                    nn = n
                    Hv = lambda a, b_, kc=kc: Hh[:, kc, a:b_]
                    Av = lambda i, a, b_, AB=AB: AB[:, i, a:b_]
                    dv = db[:, kc, 0:n]
                cur, curk = Hv, ("H", kc)
                lo, shift, i = 0, 1, 0
                while shift < w:
                    lo2 = lo + shift
                    dst = i % 2
                    if i == 0:
                        in0, in1 = Hv(lo2, L), Hv(lo2 - shift, L - shift)
                    else:
                        in0, in1 = Av(1 - dst, lo2, L), Av(1 - dst, lo2 - shift, L - shift)
                    S.add(aeng, lambda E, in0=in0, in1=in1, o_=Av(dst, lo2, L): E.tensor_tensor(out=o_, in0=in0, in1=in1, op=ALU.add),
                          [curk, MX], [(abn, dst)])
                    curk = (abn, dst)
                    lo, shift, i = lo2, shift * 2, i + 1
                last = (i - 1) % 2
                S.add("dve", lambda E, dv=dv, a_=Av(last, L - nn, L), h_=Hv(L - nn, L), w=w: E.scalar_tensor_tensor(
                    out=dv, in0=a_, scalar=1.0 / w, in1=h_, op0=ALU.mult, op1=ALU.subtract),
                    [(abn, last), ("H", kc), MX], [("db", kc)])
                if (not smpl) and t0 == 0:
                    S.add("dve", lambda E, last=last, wi=wi, AB=AB: E.tensor_tensor(
                        out=smp[:, 0, 0:16], in0=AB[:, last, 15:31], in1=cst[:, 256 + wi * 16:256 + wi * 16 + 16], op=ALU.mult),
                        [(abn, last), "cst", MX], [("smp", 0)])
                    S.add("dve", lambda E, kc=kc: E.tensor_tensor(out=db[:, kc, 0:16], in0=smp[:, 0, 0:16], in1=Hh[:, kc, 15:31], op=ALU.subtract),
                          [("smp", 0), ("H", kc), MX], [("db", kc)])
            prebias(gi, gb[:], "gb")
            for gq in range(4):
                for mo in range(2):
                    m = gq * 2 + mo
                    bank = 4 + (ecnt[0] % 2)
                    ecnt[0] += 1
                    for ki in range(2):
                        S.add("pe", lambda E, gq=gq, mo=mo, ki=ki, bank=bank: E.matmul(
                            ps[bank][:, 0:n], lhsT=wgb[:, gq, ki, mo * 128:(mo + 1) * 128], rhs=db[:, gq * 2 + ki, 0:n],
                            start=(ki == 0), stop=(ki == 1)), ["wgb", ("db", gq * 2 + ki), MX], [("ps", bank)])
                    epilogue(gi, m, bank, g2[:], "g2")
            if gi == sg[-1]:
                fence()
            ln_main(l, 1, gi)

        for gidx_, gi_ in enumerate(sg):
            _grp(gidx_, gi_)

    MIX = [mixer_gmlp, mixer_sconv, mixer_conf, mixer_pool]

    items = []
    for l in range(depth):
        for s in range(3):
            for sg in SGS:
                items.append((l, s, sg))
    ada_enqueue(0)
    ada_pump(12)
    modulate(0, 0, SGS[0])
    for it, (l, s, sg) in enumerate(items):
        nxt = items[it + 1] if it + 1 < len(items) else None
        if s == 2 and sg is SGS[0] and l + 1 < depth:
            ada_pump(36)
            ada_enqueue(l + 1)

        def after(nxt=nxt):
            if nxt is not None and not (nxt[1] == 1 and (nxt[0] % 4) == 3 and 3 in mixers):
                modulate(*nxt)
        fence_on[0] = (sg is SGS[1]) and s in (0, 1)
        if s == 1:
            k = l % 4
            if k in mixers:
                MIX[k](l, sg, after)
            else:
                after()
        else:
            ffn(l, 0 if s == 0 else 1, sg, after)
    outs = []
    for gi, (t0, n, smpl) in enumerate(groups):
        outs.append(dma("sp", o_y[:, :, t0:t0 + n], x[:, :, t0:t0 + n], xs(gi), [], f"o_y{gi}"))
    allout = [op for op in S.ops if op.grp is not None and op.grp.startswith("o_")]
    fin = S.add("sp", lambda E: E.nop(), [], [])
    fin.alldeps = [(o_, True) for o_ in allout]
    S.emit(nc, es)
    es.close()
    return nc


def _tile_w(W):
    N = W.shape[1]
    return np.ascontiguousarray(W.reshape(8, 128, N // 256, 256).transpose(2, 1, 0, 3)).reshape(N // 256, 128, 2048)


def _fm(v):
    return np.ascontiguousarray(np.asarray(v, np.float32).reshape(-1, 128).T)


def _run(inp, T, depth, mixers=(0, 1, 2, 3), ncores=8):
    f32 = lambda a: np.asarray(a, np.float32)
    NT = T + NSAMP
    pvec = np.zeros((128, NPV), np.float32)

    def put(name, v):
        a = _fm(v)
        pvec[:, PV[name]:PV[name] + a.shape[1]] = a
    put("ada_b", inp["ada_b"]); put("ln_g", inp["ln_g"]); put("ln_b", inp["ln_b"])
    put("a_b_in", inp["a_b_in"]); put("a_ln_g", inp["a_ln_g"]); put("a_ln_b", inp["a_ln_b"])
    put("b_conv", inp["b_conv"]); put("c_b_pw1", inp["c_b_pw1"])
    cdw = f32(inp["c_dw"]).reshape(31, 8, 128).transpose(2, 1, 0).reshape(128, 248)
    pvec[:, PV["c_dw"]:PV["c_dw"] + 248] = cdw
    put("c_b_dw", inp["c_b_dw"]); put("c_ln_g", inp["c_ln_g"]); put("c_ln_b", inp["c_ln_b"]); put("c_b_pw2", inp["c_b_pw2"])
    put("d_b_grp", f32(inp["d_b_grp"]).reshape(-1)); put("d_scale", inp["d_scale"])
    adaw = np.concatenate([_tile_w(f32(inp["ada_w"][l])) for l in range(DEPTH)], 0)
    wi = f32(inp["ffn_w_in"])
    win = np.empty((DEPTH * 2 * NJ, 128, 2048), np.float32)
    for l in range(DEPTH):
        for f in range(2):
            W = wi[l, f].reshape(8, 128, 2, NJ, 128)
            win[(l * 2 + f) * NJ:(l * 2 + f + 1) * NJ] = W.transpose(3, 1, 0, 2, 4).reshape(NJ, 128, 2048)
    wout = np.ascontiguousarray(f32(inp["ffn_w_out"]).reshape(DEPTH * 2 * DFF, D))
    mixw = np.concatenate([_tile_w(f32(inp[k])) for k in ("a_w_in", "a_w_out", "b_w_in", "b_w_out", "c_w_pw1", "c_w_pw2")], 0)
    ws = f32(inp["a_w_s"])
    wsT = np.zeros((128, 2, 4, 128), np.float32)
    wsT[:, 0] = ws.transpose(2, 0, 1)
    for i in range(NS):
        wsT[8 * i:8 * i + 8, 1, :, 8 * i:8 * i + 8] = ws[:, 0:8, 0:8].transpose(2, 0, 1)
    bs = f32(inp["a_b_s"])
    bsrow = np.zeros((1, 2, 512), np.float32)
    bsrow[0, 0] = bs.reshape(-1)
    bsrow[0, 1] = np.tile(bs[:, 0:8], (1, NS)).reshape(-1)
    dwg = np.ascontiguousarray(f32(inp["d_w_grp"]).reshape(4, 2, 128, 256).transpose(2, 0, 1, 3)).reshape(128, 2048)
    consts = np.zeros((128, 320), np.float32)
    consts[:, 0:128] = np.eye(128, dtype=np.float32)
    consts[:, 128:256] = np.triu(np.ones((128, 128), np.float32))
    for wi_, w in enumerate(POOL_W):
        consts[:, 256 + wi_ * 16:256 + wi_ * 16 + 16] = 1.0 / np.minimum(np.arange(16) + 1, w)
    shared = dict(pvec=pvec, adaw=adaw, win=win, wout=wout, mixw=mixw, wsT=wsT, bsrow=bsrow, dwg=dwg, consts=consts)
    xp, xsm = f32(inp["x_prompt"]), f32(inp["x_sample"])
    cp, cs = f32(inp["c_prompt"]), f32(inp["c_sample"])
    fmT = lambda a: np.ascontiguousarray(a.reshape(a.shape[0], 8, 128).transpose(2, 1, 0))
    in_maps = []
    for c in range(ncores):
        sl = slice(NS * c, NS * (c + 1))
        xT = np.concatenate([fmT(xp[c]), fmT(xsm[sl].reshape(NSAMP, D))], 2)
        cT = np.concatenate([fmT(cp[c:c + 1]), fmT(cs[sl])], 2)
        st = lambda a, r: np.ascontiguousarray(fmT(f32(a)[sl].reshape(NS * r, D)).reshape(128, 8, NS, r))
        m = dict(shared)
        m.update(xT=xT, cT=cT, st3=st(inp["state_conv3"], 2), st31=st(inp["state_conv31"], 30), stp=st(inp["state_pool"], 15))
        in_maps.append(m)
    nc = build(T, depth, mixers)
    res = run_bass_kernel_spmd(nc, in_maps, core_ids=list(range(ncores)))
    R = res.results
    tm = lambda a: np.ascontiguousarray(a.reshape(128, 8, -1).transpose(2, 1, 0)).reshape(-1, D)
    B = ncores
    y_p = np.stack([tm(R[c]["yT"][:, :, 0:T]) for c in range(B)])
    y_s = np.concatenate([tm(R[c]["yT"][:, :, T:NT]).reshape(NS, SL, D) for c in range(B)])
    v_p = np.stack([tm(R[c]["vch"][:, :, 0:128]) for c in range(B)])
    v_s = np.concatenate([tm(R[c]["vch"][:, :, 128:256]).reshape(NS, SL, D) for c in range(B)])

    def pp(name):
        return np.stack([tm(R[c][name]) for c in range(B)])

    def ss(arrs, r):
        return np.concatenate([tm(np.ascontiguousarray(a).reshape(128, 8, NS * r)).reshape(NS, r, D) for a in arrs])
    c3p = pp("oc3p")
    c3s = ss([R[c]["oc3s"].reshape(128, 8, NS, 10)[:, :, :, 8:10] for c in range(B)], 2)
    c31p = pp("oc31p")
    c31s = ss([np.concatenate([R[c]["oc31a"][:, :, :, 8:30], R[c]["oc31b"].reshape(128, 8, NS, SL)], 3) for c in range(B)], 30)
    plp = pp("oplp")
    pls = ss([R[c]["opls"].reshape(128, 8, NS, 23)[:, :, :, 8:23] for c in range(B)], 15)
    return (y_p, y_s, v_p, v_s, c3p, c3s, c31p, c31s, plp, pls)


def kernel(**inputs):
    return _run(inputs, 2048, DEPTH)
```
